# Optimizing a Trainium2 kernel written in Bass

```python
import math
import jax, jax.numpy as jnp
from jax import lax
import numpy as np

D_MODEL = 2048
BATCH = 16
SEQ = 256
DEPTH = 4
DEC_BATCH = 2
DEC_SEQ = 2048
PAST_LEN = 256

GRID_W = 64
N_MIXERS = 2
N_WIN_LAYERS = (DEPTH + 1) // 2
N_MLA_LAYERS = DEPTH // 2
WIN_HEADS = 32
WIN_KV_HEADS = 8
WIN_GROUP = WIN_HEADS // WIN_KV_HEADS
WIN_HEAD_DIM = 64
WINDOW = 128
BLOCK = WINDOW
WIN_SCALE = WIN_HEAD_DIM ** -0.5
MLA_HEADS = 16
Q_LORA_RANK = 512
KV_LORA_RANK = 512
QK_NOPE_DIM = 128
QK_ROPE_DIM = 64
V_HEAD_DIM = 128
MLA_SCALE = (QK_NOPE_DIM + QK_ROPE_DIM) ** -0.5
D_FF = -(-8 * D_MODEL // (3 * 256)) * 256
ROPE_BASE = 10000.0
EPS = 1e-6
NEG = float(np.finfo(np.float32).min)

kernel_name = "hybrid_dit_prefix_window_mla_step"


def rms_norm(x, g):
    xf = x.astype(jnp.float32)
    y = xf * lax.rsqrt(jnp.mean(xf * xf, axis=-1, keepdims=True) + EPS)
    return (y * g.astype(jnp.float32)).astype(x.dtype)


def ada_modulation(cond, w_ada, b_ada):
    m = jax.nn.silu(cond) @ w_ada + b_ada
    return jnp.split(m[:, None, :], 6, axis=-1)


def modulate(x, g, shift, scale):
    return rms_norm(x, g) * (1 + scale) + shift


def grid_positions(n_tokens):
    n_rows = n_tokens // GRID_W
    rows = jnp.broadcast_to(jnp.arange(n_rows, dtype=jnp.int32)[:, None], (n_rows, GRID_W)).reshape(-1)
    cols = jnp.broadcast_to(jnp.arange(GRID_W, dtype=jnp.int32)[None, :], (n_rows, GRID_W)).reshape(-1)
    return rows, cols


def rope_angles(pos, dim):
    half = dim // 2
    freqs = ROPE_BASE ** (-jnp.arange(half, dtype=jnp.float32) / half)
    ang = pos.astype(jnp.float32)[:, None] * freqs[None, :]
    return jnp.cos(ang), jnp.sin(ang)


def rope_1d(x, cos, sin):
    half = x.shape[-1] // 2
    x1, x2 = x[..., :half], x[..., half:]
    c = cos[:, None, :].astype(x.dtype)
    s = sin[:, None, :].astype(x.dtype)
    return jnp.concatenate([x1 * c - x2 * s, x1 * s + x2 * c], axis=-1)


def rope_2d(x, rows, cols):
    half = x.shape[-1] // 2
    cr, sr = rope_angles(rows, half)
    cc, sc = rope_angles(cols, half)
    return jnp.concatenate([rope_1d(x[..., :half], cr, sr), rope_1d(x[..., half:], cc, sc)], axis=-1)


def sink_softmax(logits, sink_logit):
    sink = jnp.broadcast_to(sink_logit, logits.shape[:-1] + (1,))
    p = jax.nn.softmax(jnp.concatenate([logits, sink], axis=-1), axis=-1)
    return p[..., :-1]


def win_qkv(h, w_qkv):
    B, L, _ = h.shape
    qd, kd = WIN_HEADS * WIN_HEAD_DIM, WIN_KV_HEADS * WIN_HEAD_DIM
    q, k, v = jnp.split(h @ w_qkv, [qd, qd + kd], axis=-1)
    return (q.reshape(B, L, WIN_HEADS, WIN_HEAD_DIM),
            k.reshape(B, L, WIN_KV_HEADS, WIN_HEAD_DIM),
            v.reshape(B, L, WIN_KV_HEADS, WIN_HEAD_DIM))


def win_attn_context(h, w_qkv, w_o, sink):
    B, L, _ = h.shape
    q, k, v = win_qkv(h, w_qkv)
    q = q.reshape(B, L, WIN_KV_HEADS, WIN_GROUP, WIN_HEAD_DIM)
    logits = jnp.einsum('bqkgd,bskd->bkgqs', q, k).astype(jnp.float32) * WIN_SCALE
    p = sink_softmax(logits, sink.reshape(WIN_KV_HEADS, WIN_GROUP)[None, :, :, None, None].astype(jnp.float32))
    o = jnp.einsum('bkgqs,bskd->bqkgd', p.astype(v.dtype), v).reshape(B, L, WIN_HEADS * WIN_HEAD_DIM)
    return o @ w_o, k, v


def win_attn_latent(h, ctx_k, ctx_v, rows, cols, w_qkv, w_o, sink):
    B, L, _ = h.shape
    nb = L // BLOCK
    q, k, v = win_qkv(h, w_qkv)
    q = rope_2d(q, rows, cols).reshape(B, nb, BLOCK, WIN_KV_HEADS, WIN_GROUP, WIN_HEAD_DIM)
    k = rope_2d(k, rows, cols)
    pad = ((0, 0), (WINDOW, WINDOW), (0, 0), (0, 0))
    kb = jnp.pad(k, pad).reshape(B, nb + 2, BLOCK, WIN_KV_HEADS, WIN_HEAD_DIM)
    vb = jnp.pad(v, pad).reshape(B, nb + 2, BLOCK, WIN_KV_HEADS, WIN_HEAD_DIM)
    kw = jnp.concatenate([kb[:, :-2], kb[:, 1:-1], kb[:, 2:]], axis=2)
    vw = jnp.concatenate([vb[:, :-2], vb[:, 1:-1], vb[:, 2:]], axis=2)
    blk_start = (jnp.arange(nb) * BLOCK)[:, None, None]
    key_pos = blk_start - WINDOW + jnp.arange(3 * BLOCK)[None, None, :]
    qry_pos = blk_start + jnp.arange(BLOCK)[None, :, None]
    valid = (jnp.abs(key_pos - qry_pos) <= WINDOW) & (key_pos >= 0) & (key_pos < L)
    lw = jnp.einsum('bnqkgd,bnskd->bnkgqs', q, kw).astype(jnp.float32) * WIN_SCALE
    lw = jnp.where(valid[None, :, None, None, :, :], lw, NEG)
    lc = jnp.einsum('bnqkgd,bckd->bnkgqc', q, ctx_k).astype(jnp.float32) * WIN_SCALE
    p = sink_softmax(jnp.concatenate([lw, lc], axis=-1),
                     sink.reshape(WIN_KV_HEADS, WIN_GROUP)[None, None, :, :, None, None].astype(jnp.float32))
    pw = p[..., :3 * BLOCK].astype(v.dtype)
    pc = p[..., 3 * BLOCK:].astype(v.dtype)
    o = (jnp.einsum('bnkgqs,bnskd->bnqkgd', pw, vw)
         + jnp.einsum('bnkgqc,bckd->bnqkgd', pc, ctx_v))
    return o.reshape(B, L, WIN_HEADS * WIN_HEAD_DIM) @ w_o


def mla_project(h, w_in, q_norm, w_q_b, kv_norm):
    B, L, _ = h.shape
    c_q, c_kv, k_pe = jnp.split(h @ w_in, [Q_LORA_RANK, Q_LORA_RANK + KV_LORA_RANK], axis=-1)
    q = (rms_norm(c_q, q_norm) @ w_q_b).reshape(B, L, MLA_HEADS, QK_NOPE_DIM + QK_ROPE_DIM)
    return q[..., :QK_NOPE_DIM], q[..., QK_NOPE_DIM:], rms_norm(c_kv, kv_norm), k_pe


def mla_expand(c_kv, w_kv_b):
    B, L, _ = c_kv.shape
    kv = (c_kv @ w_kv_b).reshape(B, L, MLA_HEADS, QK_NOPE_DIM + V_HEAD_DIM)
    return kv[..., :QK_NOPE_DIM], kv[..., QK_NOPE_DIM:]


def mla_attn_context(h, w_in, q_norm, w_q_b, kv_norm, w_kv_b, w_o):
    B, L, _ = h.shape
    q_nope, q_pe, c_kv, k_pe = mla_project(h, w_in, q_norm, w_q_b, kv_norm)
    k_nope, v = mla_expand(c_kv, w_kv_b)
    logits = (jnp.einsum('bqhd,bshd->bhqs', q_nope, k_nope)
              + jnp.einsum('bqhr,bsr->bhqs', q_pe, k_pe)).astype(jnp.float32) * MLA_SCALE
    p = jax.nn.softmax(logits, axis=-1).astype(v.dtype)
    o = jnp.einsum('bhqs,bshd->bqhd', p, v).reshape(B, L, MLA_HEADS * V_HEAD_DIM)
    return o @ w_o, c_kv, k_pe


def mla_attn_latent(h, ctx_ckv, ctx_kpe, rows, cols, w_in, q_norm, w_q_b, kv_norm, w_kv_b, w_o):
    B, L, _ = h.shape
    nb = L // BLOCK
    q_nope, q_pe, c_kv, k_pe = mla_project(h, w_in, q_norm, w_q_b, kv_norm)
    q_pe = rope_2d(q_pe, rows, cols)
    k_pe = rope_2d(k_pe[:, :, None, :], rows, cols)[:, :, 0, :]
    k_nope, v = mla_expand(jnp.concatenate([c_kv, ctx_ckv], axis=1), w_kv_b)
    k_pe = jnp.concatenate([k_pe, ctx_kpe], axis=1)
    qn_b = q_nope.reshape(B, nb, BLOCK, MLA_HEADS, QK_NOPE_DIM).transpose(1, 0, 2, 3, 4)
    qp_b = q_pe.reshape(B, nb, BLOCK, MLA_HEADS, QK_ROPE_DIM).transpose(1, 0, 2, 3, 4)

    def attend_block(qb):
        qn, qp = qb
        logits = (jnp.einsum('bqhd,bshd->bhqs', qn, k_nope)
                  + jnp.einsum('bqhr,bsr->bhqs', qp, k_pe)).astype(jnp.float32) * MLA_SCALE
        p = jax.nn.softmax(logits, axis=-1).astype(v.dtype)
        return jnp.einsum('bhqs,bshd->bqhd', p, v)

    o = lax.map(attend_block, (qn_b, qp_b))
    o = o.transpose(1, 0, 2, 3, 4).reshape(B, L, MLA_HEADS * V_HEAD_DIM)
    return o @ w_o


def swiglu(h, w_gate, w_up, w_down):
    return (jax.nn.silu(h @ w_gate) * (h @ w_up)) @ w_down


def setup_inputs(seed: int = 0) -> dict:
    key = jax.random.key(seed)
    ks = iter(jax.random.split(key, 32))

    def nrm(shape, scale):
        return jax.random.normal(next(ks), shape, jnp.float32) * scale

    def gain(shape):
        return 1.0 + nrm(shape, 0.1)

    qkv_w = (WIN_HEADS + 2 * WIN_KV_HEADS) * WIN_HEAD_DIM
    return {
        "x_prompt": nrm((BATCH, SEQ, D_MODEL), 1.0),
        "x_sample": nrm((DEC_BATCH, DEC_SEQ, D_MODEL), 1.0),
        "cache_win_k": nrm((DEC_BATCH, N_WIN_LAYERS, PAST_LEN, WIN_KV_HEADS, WIN_HEAD_DIM), 1.0),
        "cache_win_v": nrm((DEC_BATCH, N_WIN_LAYERS, PAST_LEN, WIN_KV_HEADS, WIN_HEAD_DIM), 1.0),
        "cache_mla_ckv": nrm((DEC_BATCH, N_MLA_LAYERS, PAST_LEN, KV_LORA_RANK), 1.0),
        "cache_mla_kpe": nrm((DEC_BATCH, N_MLA_LAYERS, PAST_LEN, QK_ROPE_DIM), 1.0),
        "c": nrm((DEC_BATCH, D_MODEL), 1.0),
        "c_ctx": nrm((D_MODEL,), 1.0),
        "ada_w": nrm((DEPTH, D_MODEL, 6 * D_MODEL), 0.5 * D_MODEL ** -0.5),
        "ada_b": nrm((DEPTH, 6 * D_MODEL), 0.02),
        "norm_mix": gain((DEPTH, D_MODEL)),
        "norm_ffn": gain((DEPTH, D_MODEL)),
        "win_w_qkv": nrm((N_WIN_LAYERS, D_MODEL, qkv_w), D_MODEL ** -0.5),
        "win_w_o": nrm((N_WIN_LAYERS, WIN_HEADS * WIN_HEAD_DIM, D_MODEL), (WIN_HEADS * WIN_HEAD_DIM) ** -0.5),
        "win_sink": nrm((N_WIN_LAYERS, WIN_HEADS), 0.5),
        "mla_w_in": nrm((N_MLA_LAYERS, D_MODEL, Q_LORA_RANK + KV_LORA_RANK + QK_ROPE_DIM), D_MODEL ** -0.5),
        "mla_q_norm": gain((N_MLA_LAYERS, Q_LORA_RANK)),
        "mla_w_q_b": nrm((N_MLA_LAYERS, Q_LORA_RANK, MLA_HEADS * (QK_NOPE_DIM + QK_ROPE_DIM)), Q_LORA_RANK ** -0.5),
        "mla_kv_norm": gain((N_MLA_LAYERS, KV_LORA_RANK)),
        "mla_w_kv_b": nrm((N_MLA_LAYERS, KV_LORA_RANK, MLA_HEADS * (QK_NOPE_DIM + V_HEAD_DIM)), KV_LORA_RANK ** -0.5),
        "mla_w_o": nrm((N_MLA_LAYERS, MLA_HEADS * V_HEAD_DIM, D_MODEL), (MLA_HEADS * V_HEAD_DIM) ** -0.5),
        "ffn_w_gate": nrm((DEPTH, D_MODEL, D_FF), D_MODEL ** -0.5),
        "ffn_w_up": nrm((DEPTH, D_MODEL, D_FF), D_MODEL ** -0.5),
        "ffn_w_down": nrm((DEPTH, D_FF, D_MODEL), D_FF ** -0.5),
        "norm_final": gain((D_MODEL,)),
    }


def reference(x_prompt, x_sample, cache_win_k, cache_win_v, cache_mla_ckv, cache_mla_kpe, c,
              c_ctx, ada_w, ada_b, norm_mix, norm_ffn, win_w_qkv, win_w_o, win_sink,
              mla_w_in, mla_q_norm, mla_w_q_b, mla_kv_norm, mla_w_kv_b, mla_w_o,
              ffn_w_gate, ffn_w_up, ffn_w_down, norm_final):
    rows, cols = grid_positions(x_sample.shape[1])
    xc, xs = x_prompt, x_sample
    win_k_out, win_v_out, ckv_out, kpe_out = [], [], [], []
    for layer in range(DEPTH):
        j = layer // N_MIXERS
        sh_mc, sc_mc, g_mc, sh_fc, sc_fc, g_fc = ada_modulation(c_ctx[None, :], ada_w[layer], ada_b[layer])
        sh_ms, sc_ms, g_ms, sh_fs, sc_fs, g_fs = ada_modulation(c, ada_w[layer], ada_b[layer])
        hc = modulate(xc, norm_mix[layer], sh_mc, sc_mc)
        hs = modulate(xs, norm_mix[layer], sh_ms, sc_ms)
        if layer % N_MIXERS == 0:
            oc, k_ctx, v_ctx = win_attn_context(hc, win_w_qkv[j], win_w_o[j], win_sink[j])
            os_ = win_attn_latent(hs, cache_win_k[:, j], cache_win_v[:, j], rows, cols,
                                  win_w_qkv[j], win_w_o[j], win_sink[j])
            win_k_out.append(k_ctx)
            win_v_out.append(v_ctx)
        else:
            oc, ckv_ctx, kpe_ctx = mla_attn_context(hc, mla_w_in[j], mla_q_norm[j], mla_w_q_b[j],
                                                    mla_kv_norm[j], mla_w_kv_b[j], mla_w_o[j])
            os_ = mla_attn_latent(hs, cache_mla_ckv[:, j], cache_mla_kpe[:, j], rows, cols,
                                  mla_w_in[j], mla_q_norm[j], mla_w_q_b[j],
                                  mla_kv_norm[j], mla_w_kv_b[j], mla_w_o[j])
            ckv_out.append(ckv_ctx)
            kpe_out.append(kpe_ctx)
        xc = xc + g_mc * oc
        xs = xs + g_ms * os_
        hc = modulate(xc, norm_ffn[layer], sh_fc, sc_fc)
        hs = modulate(xs, norm_ffn[layer], sh_fs, sc_fs)
        xc = xc + g_fc * swiglu(hc, ffn_w_gate[layer], ffn_w_up[layer], ffn_w_down[layer])
        xs = xs + g_fs * swiglu(hs, ffn_w_gate[layer], ffn_w_up[layer], ffn_w_down[layer])
    y_prompt = rms_norm(xc, norm_final)
    y_sample = rms_norm(xs, norm_final)
    new_win_k = jnp.stack(win_k_out, axis=1)
    new_win_v = jnp.stack(win_v_out, axis=1)
    new_mla_ckv = jnp.stack(ckv_out, axis=1)
    new_mla_kpe = jnp.stack(kpe_out, axis=1)
    return (y_prompt, y_sample, new_win_k, new_win_v, new_mla_ckv, new_mla_kpe)
```

```python
import numpy as np
import concourse.bass as bass
import concourse.mybir as mybir
from concourse.bass_utils import run_bass_kernel_spmd

F32 = mybir.dt.float32
BF16 = mybir.dt.bfloat16
AF = mybir.ActivationFunctionType
ALU = mybir.AluOpType

D = 2048
NCH = 16
TG = 512
DFF = 5632
NFC = 44
EPS = 1e-6
WIN_SCALE = 64 ** -0.5
MLA_SCALE = 192 ** -0.5
NEGM = -30000.0
ENGS = ["pe", "act", "dve", "pool", "sp"]
DEBUG = False
LAYERS = [0, 1, 2, 3]
DO_MIX = True
DO_FFN = True
WIN_STOP = 99
VVAR = 2


ALL_TILES = []


class Tile:
    __slots__ = ("name", "lw", "rd", "excl")

    def __init__(self, name, excl=False):
        self.name = name
        self.lw = None
        self.rd = {}
        self.excl = excl
        ALL_TILES.append(self)


class Plan:
    def __init__(self, nc):
        self.nc = nc
        self.acts = {e: [] for e in ENGS}
        self.sems = {}
        self.seq = {e: 0 for e in ENGS}
        self.known = {e: {} for e in ENGS}
        self.pending = {e: [] for e in ENGS}
        for e in ENGS:
            self.sems[("e", e)] = nc.alloc_semaphore("sem_" + e)
        self.dring = {}
        self.dpos = {}
        self.dcount = {}
        for q, n in (("sp", 10), ("pool", 6)):
            keys = []
            for i in range(n):
                k = ("d", q, i)
                self.sems[k] = nc.alloc_semaphore("dsem_%s_%d" % (q, i))
                self.dcount[k] = 0
                keys.append(k)
            self.dring[q] = keys
            self.dpos[q] = 0
        self.cccount = 0
        self.eph = {}
        self.ninst = {e: 0 for e in ENGS}

    def _deps(self, reads, writes):
        deps = []
        for t in reads:
            if t.lw is not None:
                deps.append(t.lw)
            if t.excl:
                deps.extend(t.rd.values())
        for t in writes:
            if t.lw is not None:
                deps.append(t.lw)
            deps.extend(t.rd.values())
        return deps

    def wait_deps(self, E, deps):
        deps = list(deps) + self.pending[E]
        self.pending[E] = []
        need = {}
        for k, v in deps:
            if need.get(k, 0) < v:
                need[k] = v
        for k, v in need.items():
            if self.known[E].get(k, 0) < v:
                sem = self.sems[k]
                self.acts[E].append(lambda e, sem=sem, v=v: e.wait_ge(sem, v))
                self.known[E][k] = v
                self.ninst[E] += 1

    def op(self, E, fn, reads=(), writes=()):
        self.wait_deps(E, self._deps(reads, writes))
        self.seq[E] += 1
        sem = self.sems[("e", E)]
        self.acts[E].append(lambda e, fn=fn, sem=sem: fn(e).then_inc(sem, 1))
        self.ninst[E] += 1
        me = (("e", E), self.seq[E])
        for t in reads:
            t.rd[("e", E)] = me
        for t in writes:
            t.lw = me
            t.rd = {}

    def dma(self, Q, out, in_, reads=(), writes=(), eph=True, **kw):
        ring = self.dring[Q]
        key = ring[self.dpos[Q] % len(ring)]
        self.dpos[Q] += 1
        prev = self.dcount[key]
        deps = self._deps(reads, writes)
        if prev:
            deps.append((key, 16 * prev))
        self.wait_deps(Q, deps)
        self.dcount[key] = prev + 1
        sem = self.sems[key]
        self.acts[Q].append(lambda e, sem=sem, out=out, in_=in_, kw=kw: e.dma_start(out=out, in_=in_, **kw).then_inc(sem, 16))
        self.ninst[Q] += 1
        me = (key, 16 * (prev + 1))
        if eph:
            self.eph[key] = me[1]
        for t in reads:
            t.rd[key] = me
        for t in writes:
            t.lw = me
            t.rd = {}

    def allgather(self, in_ap, out_ap, groups, reads, writes):
        deps = self._deps(reads, writes)
        if self.cccount:
            deps.append((("cc", self.cccount - 1), 1))
        self.wait_deps("pool", deps)
        key = ("cc", self.cccount)
        self.sems[key] = self.nc.alloc_semaphore("ccsem%d" % self.cccount)
        self.cccount += 1
        sem = self.sems[key]
        self.acts["pool"].append(
            lambda e: e.collective_compute("AllGather", ALU.bypass, replica_groups=groups, ins=[in_ap], outs=[out_ap]).then_inc(sem, 1)
        )
        me = (key, 1)
        self.eph[key] = 1
        for t in reads:
            t.rd[key] = me
        for t in writes:
            t.lw = me
            t.rd = {}

    def add_pending(self, deps):
        deps = list(deps)
        for e in ENGS:
            self.pending[e] = self.pending[e] + deps

    def fence(self):
        deps = [(("e", e), self.seq[e]) for e in ENGS if self.seq[e]]
        deps += list(self.eph.items())
        for e in ENGS:
            self.pending[e] = self.pending[e] + deps

    def finish(self):
        deps = [(k, 16 * c) for k, c in self.dcount.items() if c]
        deps += [(("e", e), self.seq[e]) for e in ENGS if self.seq[e]]
        for i in range(self.cccount):
            deps.append((("cc", i), 1))
        self.wait_deps("sp", deps)

    def emit(self):
        nc = self.nc
        acts = self.acts
        with nc.Block() as block:
            @block.tensor
            def _(e):
                for f in acts["pe"]:
                    f(e)

            @block.scalar
            def _(e):
                for f in acts["act"]:
                    f(e)

            @block.vector
            def _(e):
                for f in acts["dve"]:
                    f(e)

            @block.gpsimd
            def _(e):
                for f in acts["pool"]:
                    f(e)

            @block.sync
            def _(e):
                for f in acts["sp"]:
                    f(e)


def _counts():
    nwin = max([l // 2 + 1 for l in LAYERS if l % 2 == 0 and DO_MIX] + [0])
    nmla = max([l // 2 + 1 for l in LAYERS if l % 2 == 1 and DO_MIX] + [0])
    nffn = max([l + 1 for l in LAYERS if DO_FFN] + [0])
    return nwin, nmla, nffn


def rope_tables(core):
    q = core % 4
    t = q * 512 + np.arange(512)
    rows = (t // 64).astype(np.float32)
    cols = (t % 64).astype(np.float32)
    freqs = (10000.0 ** (-np.arange(16, dtype=np.float32) / 16)).astype(np.float32)
    cosT = np.zeros((128, 512), np.float32)
    sinT = np.zeros((128, 512), np.float32)
    for p in range(128):
        d = p % 64
        pos = rows if d < 32 else cols
        ang = (pos * freqs[d % 16]).astype(np.float32)
        cosT[p] = np.cos(ang)
        sgn = -1.0 if (d % 32) < 16 else 1.0
        sinT[p] = sgn * np.sin(ang)
    return cosT, sinT


def mask_tables(core):
    q = core % 4
    s = np.arange(128)[:, None]
    t = np.arange(128)[None, :]
    mA = np.where(t <= s, 0.0, NEGM).astype(np.float32)
    mB = np.where(s <= t, 0.0, NEGM).astype(np.float32)
    mL = mA if q > 0 else np.full((128, 128), NEGM, np.float32)
    mR = mB if q < 3 else np.full((128, 128), NEGM, np.float32)
    sel = np.zeros((128, 10), np.float32)
    sel[:, 8 + core // 4] = 1.0
    if q > 0:
        sel[:, q - 1] = 1.0
    if q < 3:
        sel[:, 4 + q + 1] = 1.0
    return np.stack([mA, mB, mL, mR]), sel


def build():
    nc = bass.Bass("TRN2", target_bir_lowering=False)
    P = Plan(nc)

    def din(name, shape):
        return nc.dram_tensor(name, list(shape), F32, kind="ExternalInput").ap()

    def dout(name, shape):
        return nc.dram_tensor(name, list(shape), F32, kind="ExternalOutput").ap()

    xp = din("xp", [512, D])
    xs = din("xs", [512, D])
    vecs = din("vecs", [12, D])
    nvec = din("nvec", [4, 512])
    ada_w = din("ada_w", [4, D, 1536])
    ada_b = din("ada_b", [48, 128])
    nwin, nmla, nffn = _counts()

    def dinw(name, n, shape):
        return din(name, [n] + shape) if n else din(name, [1, 1, 1])

    w_qkv = dinw("win_w_qkv", nwin, [D, 3072])
    w_wo = dinw("win_w_o", nwin, [D, D])
    sink = din("win_sink", [2, 32])
    m_in = dinw("mla_w_in", nmla, [D, 1088])
    m_qb = dinw("mla_w_q_b", nmla, [512, 3072])
    m_kvb = dinw("mla_w_kv_b", nmla, [512, 4096])
    m_wo = dinw("mla_w_o", nmla, [D, D])
    f_g = dinw("ffn_w_gate", nffn, [D, DFF])
    f_u = dinw("ffn_w_up", nffn, [D, DFF])
    f_d = dinw("ffn_w_down", nffn, [DFF, D])
    c_wk = din("c_wk", [2, 256, 512])
    c_wv = din("c_wv", [2, 256, 512])
    c_ckv = din("c_ckv", [2, 256, 512])
    c_kpe = din("c_kpe", [2, 256, 64])
    ropec = din("ropec", [128, 512])
    ropes = din("ropes", [128, 512])
    masks = din("masks", [4, 128, 128])
    seld = din("sel", [128, 10])

    yp = dout("yp", [512, D])
    ys = dout("ys", [512, D])
    o_wk = dout("o_wk", [2, 2, 256, 512])
    o_wv = dout("o_wv", [2, 2, 256, 512])
    o_ckv = dout("o_ckv", [2, 2, 256, 512])
    o_kpe = dout("o_kpe", [2, 2, 256, 64])
    dbg = dout("dbg", [8, 128, 16 * 1024]) if DEBUG else None

    b_mod = nc.dram_tensor("b_mod", [128, 144], F32).ap()
    g_mod = nc.dram_tensor("g_mod", [8 * 128, 144], F32).ap()
    b_win = [nc.dram_tensor("b_win%d" % i, [128, 2048], BF16).ap() for i in range(2)]
    g_win = [nc.dram_tensor("g_win%d" % i, [4 * 128, 2048], BF16).ap() for i in range(2)]
    b_mla = [nc.dram_tensor("b_mla%d" % i, [128, 2560], BF16).ap() for i in range(2)]
    g_mla = [nc.dram_tensor("g_mla%d" % i, [4 * 128, 2560], BF16).ap() for i in range(2)]
    T_bmod, T_gmod = Tile("bmod"), Tile("gmod")
    T_bwin = [Tile("bwin%d" % i) for i in range(2)]
    T_gwin = [Tile("gwin%d" % i) for i in range(2)]
    T_bmla = [Tile("bmla%d" % i) for i in range(2)]
    T_gmla = [Tile("gmla%d" % i) for i in range(2)]
    G8 = [list(range(8))]
    G4 = [[0, 1, 2, 3], [4, 5, 6, 7]]

    SB0 = (nc.sbuf_base + 31) // 32 * 32
    SBTOP = nc.sbuf_top
    ALL_TILES.clear()
    mem = {"p": SB0}

    def salloc(name, shape, dt, base=None):
        size = int(np.prod(shape[1:])) * (2 if dt == BF16 else 4)
        size = (size + 31) // 32 * 32
        if base is None:
            off = mem["p"]
            mem["p"] += size
            assert mem["p"] <= SBTOP, ("SBUF overflow", name, mem["p"], SBTOP)
        else:
            off = base
        return nc.alloc_sbuf_tensor_at(name, list(shape), dt, offset=off), size

    uid = [0]

    def nm(s):
        uid[0] += 1
        return "%s_%d" % (s, uid[0])

    ident_f, _ = salloc("ident_f", [128, 128], F32)
    ident_b, _ = salloc("ident_b", [128, 128], BF16)
    ones_b, _ = salloc("ones_b", [128, 128], BF16)
    cosT, _ = salloc("cosT", [128, 512], F32)
    sinT, _ = salloc("sinT", [128, 512], F32)
    mk, _ = salloc("mk", [128, 4, 128], BF16)
    selT, _ = salloc("selT", [128, 10], F32)
    vT, _ = salloc("vT", [128, 16, 12], F32)
    nT, _ = salloc("nT", [128, 4, 4], F32)
    esink, _ = salloc("esink", [128, 2, 32], F32)
    MOD, _ = salloc("MOD", [128, 4, 3, 8, 12], F32)
    AMF, _ = salloc("AMF", [128, 4, 2, 2, 16], F32)
    xT, _ = salloc("xT", [128, 16, 1024], F32)
    hT, _ = salloc("hT", [128, 2, 16, 512], BF16)
    WS = [salloc("ws%d" % i, [128, 4096], BF16)[0] for i in range(4)]
    RA0 = mem["p"]
    RA_SIZE = SBTOP - RA0
    T_const = Tile("const")
    XT = [[Tile("xt%d_%d" % (c, g)) for g in range(2)] for c in range(16)]
    HT = [[Tile("ht%d_%d" % (h, c)) for c in range(16)] for h in range(2)]
    WST = [Tile("ws%d" % i) for i in range(4)]
    PB = [nc.alloc_psum_tensor("pb%d" % i, [128, 512], F32) for i in range(8)]
    PBT = [Tile("pb%d" % i, excl=True) for i in range(8)]

    ra = {"p": RA0, "stack": []}

    def ra_push():
        ra["stack"].append((ra["p"], len(ALL_TILES)))

    def ra_pop():
        ra["p"], n0 = ra["stack"].pop()
        deps = []
        for t in ALL_TILES[n0:]:
            if t.lw is not None:
                deps.append(t.lw)
            deps.extend(t.rd.values())
        P.add_pending(deps)

    def ralloc(name, shape, dt):
        t, size = salloc(nm(name), shape, dt, base=ra["p"])
        ra["p"] += size
        assert ra["p"] <= SBTOP, ("RA overflow", name, ra["p"] - RA0, RA_SIZE)
        return t

    def xt_ap(ch, g):
        return xT[:, ch, g * 512:(g + 1) * 512]

    def ht_ap(h, ch):
        return hT[:, h, ch, :]

    class Ring:
        def __init__(self, idx):
            self.idx = list(idx)
            self.i = 0

        def next(self):
            b = self.idx[self.i % len(self.idx)]
            self.i += 1
            return PB[b], PBT[b]

    wq = {"specs": [], "i": 0, "issued": 0, "collect": True}

    def wget(srcs, k, n, keep=0):
        i = wq["i"]
        wq["i"] += 1
        if wq["collect"]:
            wq["specs"].append((srcs, k, n))
            return None, None
        assert keep <= 3
        limit = min(len(wq["specs"]), i - keep + 4)
        while wq["issued"] < limit:
            ii = wq["issued"]
            s_srcs, s_k, s_n = wq["specs"][ii]
            slot = WS[ii % 4]
            view = slot[:, 0:s_k * s_n].rearrange("p (k n) -> p k n", k=s_k)
            for (src, c0, cw) in s_srcs:
                P.dma("pool", view[:, :, c0:c0 + cw], src, writes=[WST[ii % 4]], eph=False)
            wq["issued"] += 1
        assert wq["issued"] > i
        slot = WS[i % 4]
        return slot[:, 0:k * n].rearrange("p (k n) -> p k n", k=k), WST[i % 4]

    def wview(w2d, c0, cw):
        return w2d.rearrange("(k p) n -> p k n", p=128)[:, :, c0:c0 + cw]

    def mm_group(items):
        def fn(e):
            r = None
            for (o, l, r_, st, sp_) in items:
                r = e.matmul(o, lhsT=l, rhs=r_, start=st, stop=sp_)
            return r
        return fn

    def act_fn(out, in_, func, **kw):
        return lambda e: e.activation(out=out, in_=in_, func=func, **kw)

    def copy_any(E, out, in_, reads, writes):
        if E == "act":
            P.op("act", lambda e: e.copy(out=out, in_=in_), reads, writes)
        else:
            P.op(E, lambda e: e.tensor_copy(out=out, in_=in_), reads, writes)

    def build_body():
        bank_all = Ring(range(8))

        P.op("pool", lambda e: e.memset(ident_f[:], 0.0), writes=[T_const])
        P.op("pool", lambda e: e.affine_select(out=ident_f[:], in_=ident_f[:], pattern=[[-1, 128]], compare_op=ALU.not_equal,
                                                fill=1.0, base=0, channel_multiplier=1), writes=[T_const])
        P.op("pool", lambda e: e.memset(ones_b[:], 1.0), writes=[T_const])
        P.op("dve", lambda e: e.tensor_copy(out=ident_b[:], in_=ident_f[:]), reads=[T_const], writes=[T_const])
        T_tab = Tile("tab")
        P.dma("sp", cosT[:], ropec, writes=[T_tab], eph=False)
        P.dma("sp", sinT[:], ropes, writes=[T_tab], eph=False)
        P.dma("sp", selT[:], seld, writes=[T_tab], eph=False)
        P.dma("pool", mk[:], masks.rearrange("m p n -> p m n"), writes=[T_tab], eph=False)
        T_es = Tile("esink")
        P.dma("sp", esink[:].rearrange("p a b -> p (a b)"), sink.rearrange("a b -> (a b)").partition_broadcast(128), writes=[T_es], eph=False)
        P.op("act", act_fn(esink[:], esink[:], AF.Exp), writes=[T_es])

        ra_push()
        v11 = ralloc("v11", [12, D], F32)
        n4 = ralloc("n4", [4, 512], F32)
        ab48 = ralloc("ab48", [48, 128], F32)
        T_v11, T_n4, T_ab = Tile("v11"), Tile("n4"), Tile("ab48")
        P.dma("sp", v11[:], vecs, writes=[T_v11])
        P.dma("sp", n4[:], nvec, writes=[T_n4])
        P.dma("sp", ab48[:], ada_b, writes=[T_ab])
        T_vT = Tile("vT")
        for q4 in range(4):
            pb, pt = bank_all.next()
            items = []
            for cc in range(4):
                ch = q4 * 4 + cc
                items.append((pb[:, cc * 12:(cc + 1) * 12], v11[:, ch * 128:(ch + 1) * 128], ident_f[0:12, 0:12]))
            P.op("pe", lambda e, items=items: [e.transpose(o, i, idn) for (o, i, idn) in items][-1], reads=[T_v11, T_const], writes=[pt])
            P.op("dve", lambda e, pb=pb, q4=q4: e.tensor_copy(out=vT[:, q4 * 4:(q4 + 1) * 4, :], in_=pb[:, 0:48].rearrange("p (c v) -> p c v", c=4)),
                 reads=[pt], writes=[T_vT])
        pb, pt = bank_all.next()
        items = [(pb[:, cc * 4:(cc + 1) * 4], n4[:, cc * 128:(cc + 1) * 128], ident_f[0:4, 0:4]) for cc in range(4)]
        P.op("pe", lambda e, items=items: [e.transpose(o, i, idn) for (o, i, idn) in items][-1], reads=[T_n4, T_const], writes=[pt])
        P.op("dve", lambda e, pb=pb: e.tensor_copy(out=nT[:], in_=pb[:, 0:16].rearrange("p (c v) -> p c v", c=4)), reads=[pt], writes=[T_vT])
        abT = ralloc("abT", [128, 48], F32)
        T_abT = Tile("abT")
        pb, pt = bank_all.next()
        P.op("pe", lambda e, pb=pb: e.transpose(pb[:, 0:48], ab48[:, :], ident_f[0:48, 0:48]), reads=[T_ab, T_const], writes=[pt])
        P.op("dve", lambda e, pb=pb: e.tensor_copy(out=abT[:], in_=pb[:, 0:48]), reads=[pt], writes=[T_abT])
        scT = ralloc("scT", [128, 16, 3], BF16)
        T_sc = Tile("scT")
        P.op("act", act_fn(scT[:], vT[:, :, 9:12], AF.Silu), reads=[T_vT], writes=[T_sc])

        pbm, ptm = bank_all.next()
        for l in range(4):
            for b6 in range(6):
                wv_, wt = wget([(wview(ada_w[l], b6 * 256, 256), 0, 256)], 16, 256)
                if wv_ is None:
                    continue
                for jj in range(2):
                    j = b6 * 2 + jj
                    col = (l * 12 + j) * 3
                    items = [(pbm[:, col:col + 3], wv_[:, k, jj * 128:(jj + 1) * 128], scT[:, k, :], k == 0, k == 15) for k in range(16)]
                    P.op("pe", mm_group(items), reads=[wt, T_sc], writes=[ptm])
        modloc = ralloc("modloc", [128, 4, 3, 12], F32)
        T_ml = Tile("modloc")
        for l in range(4):
            for c in range(3):
                P.op("dve", lambda e, l=l, c=c: e.tensor_tensor(
                    out=modloc[:, l, c, :], in0=pbm[:, l * 36:(l + 1) * 36].rearrange("p (j c) -> p j c", c=3)[:, :, c],
                    in1=abT[:, l * 12:(l + 1) * 12], op=ALU.add), reads=[ptm, T_abT], writes=[T_ml])
        P.dma("sp", b_mod, modloc[:].rearrange("p l c j -> p (l c j)"), reads=[T_ml], writes=[T_bmod])
        P.allgather(b_mod, g_mod, G8, reads=[T_bmod], writes=[T_gmod])
        T_MOD = Tile("MOD")
        for l in range(4):
            for c in range(3):
                src = g_mod.rearrange("(r p) (l c j) -> p l c r j", p=128, l=4, c=3)[:, l, c, :, :]
                P.dma("sp", MOD[:, l, c, :, :], src, reads=[T_gmod], writes=[T_MOD], eph=False)
        for l in range(4):
            m1 = MOD[:, l, 1, :, :].rearrange("p r j -> p (r j)")
            m2 = MOD[:, l, 2, :, :].rearrange("p r j -> p (r j)")
            P.op("dve", lambda e, m1=m1: e.tensor_scalar(out=m1, in0=m1, scalar1=selT[:, 8:9], scalar2=None, op0=ALU.mult),
                 reads=[T_tab], writes=[T_MOD])
            P.op("dve", lambda e, m1=m1, m2=m2: e.scalar_tensor_tensor(out=m1, in0=m2, scalar=selT[:, 9:10], in1=m1, op0=ALU.mult, op1=ALU.add),
                 reads=[T_tab], writes=[T_MOD])
        T_AMF = Tile("AMF")
        for l in range(4):
            for c in range(2):
                modf = MOD[:, l, c, :, :].rearrange("p r j -> p (r j)")
                P.op("dve", lambda e, l=l, c=c, modf=modf: e.scalar_tensor_tensor(
                    out=AMF[:, l, c, 0, :], in0=modf[:, 16:32], scalar=1.0, in1=vT[:, :, l], op0=ALU.add, op1=ALU.mult),
                    reads=[T_MOD, T_vT], writes=[T_AMF])
                P.op("dve", lambda e, l=l, c=c, modf=modf: e.scalar_tensor_tensor(
                    out=AMF[:, l, c, 1, :], in0=modf[:, 64:80], scalar=1.0, in1=vT[:, :, 4 + l], op0=ALU.add, op1=ALU.mult),
                    reads=[T_MOD, T_vT], writes=[T_AMF])

        def modv(l, c, v, ch):
            gc = v * 16 + ch
            return MOD[:, l, c, gc // 12, gc % 12:gc % 12 + 1]

        xst = [ralloc("xst%d" % i, [128, D], F32) for i in range(2)]
        T_xst = [Tile("xst%d" % i) for i in range(2)]
        ti = 0
        for g, src in ((0, xp), (1, xs)):
            for tt in range(4):
                st, stt = xst[ti % 2], T_xst[ti % 2]
                ti += 1
                P.dma("sp", st[:], src[tt * 128:(tt + 1) * 128, :], writes=[stt])
                for q4 in range(4):
                    pb, pt = bank_all.next()
                    items = [(pb[:, cc * 128:(cc + 1) * 128], st[:, (q4 * 4 + cc) * 128:(q4 * 4 + cc + 1) * 128], ident_f[:]) for cc in range(4)]
                    P.op("pe", lambda e, items=items: [e.transpose(o, i, idn) for (o, i, idn) in items][-1], reads=[stt, T_const], writes=[pt])
                    outap = xT[:, q4 * 4:(q4 + 1) * 4, g * 512 + tt * 128:g * 512 + (tt + 1) * 128]
                    inap = pb[:].rearrange("p (c t) -> p c t", c=4)
                    E = "dve" if (q4 % 2 == 0) else "act"
                    copy_any(E, outap, inap, [pt], [XT[q4 * 4 + cc][g] for cc in range(4)])
        ra_pop()
        P.fence()

        def norm_mod(groups, hh_of, Aap, Bap):
            ra_push()
            sq = [ralloc("sq%d" % i, [128, 512], BF16) for i in range(4)]
            T_sq = [Tile("sq%d" % i) for i in range(4)]
            tmp = [ralloc("ntmp%d" % i, [128, 512], F32) for i in range(3)]
            T_tmp = [Tile("ntmp%d" % i) for i in range(3)]
            rs = ralloc("rs", [128, 512], F32)
            T_rs = Tile("rs")
            ring = Ring([0, 1, 2, 3])
            for g in groups:
                pbs, pts = ring.next()
                for ch in range(16):
                    s, st_ = sq[ch % 4], T_sq[ch % 4]
                    P.op("act", act_fn(s[:], xt_ap(ch, g), AF.Square), reads=[XT[ch][g]], writes=[st_])
                    P.op("pe", mm_group([(pbs[:], ones_b[:], s[:], ch == 0, ch == 15)]), reads=[st_, T_const], writes=[pts])
                P.op("act", act_fn(rs[:], pbs[:], AF.Sqrt, scale=1.0 / D, bias=EPS), reads=[pts], writes=[T_rs])
                pbr, ptr = ring.next()
                P.op("dve", lambda e, pbr=pbr: e.reciprocal(out=pbr[:], in_=rs[:]), reads=[T_rs], writes=[ptr])
                hh = hh_of[g]
                for ch in range(16):
                    t_, tt_ = tmp[ch % 3], T_tmp[ch % 3]
                    P.op("dve", lambda e, t_=t_, ch=ch, g=g, pbr=pbr: e.scalar_tensor_tensor(
                        out=t_[:], in0=xt_ap(ch, g), scalar=Aap(g, ch), in1=pbr[:], op0=ALU.mult, op1=ALU.mult),
                        reads=[XT[ch][g], ptr, T_AMF], writes=[tt_])
                    P.op("act", act_fn(ht_ap(hh, ch), t_[:], AF.Identity, bias=Bap(g, ch), scale=1.0), reads=[tt_, T_MOD], writes=[HT[hh][ch]])
            ra_pop()

        def out_proj(w2d, g, hh, gate_ap):
            ring = Ring([4, 5, 6, 7])
            for b8 in range(8):
                wv_, wt = wget([(wview(w2d, b8 * 256, 256), 0, 256)], 16, 256)
                if wv_ is None:
                    continue
                for jj in range(2):
                    dc = b8 * 2 + jj
                    pb, pt = ring.next()
                    items = [(pb[:], wv_[:, k, jj * 128:(jj + 1) * 128], ht_ap(hh, k), k == 0, k == 15) for k in range(16)]
                    P.op("pe", mm_group(items), reads=[wt] + HT[hh], writes=[pt])
                    P.op("dve", lambda e, pb=pb, dc=dc: e.scalar_tensor_tensor(
                        out=xt_ap(dc, g), in0=pb[:], scalar=gate_ap(dc), in1=xt_ap(dc, g), op0=ALU.mult, op1=ALU.add),
                        reads=[pt, T_MOD], writes=[XT[dc][g]])

        def rope_evac(pb, pt, out_ap, out_tiles, rtmp, T_rtmp, ri, nparts=128):
            a, ta = rtmp[(2 * ri) % len(rtmp)], T_rtmp[(2 * ri) % len(rtmp)]
            b, tb = rtmp[(2 * ri + 1) % len(rtmp)], T_rtmp[(2 * ri + 1) % len(rtmp)]
            P.op("dve", lambda e: e.tensor_tensor(out=a[:], in0=pb[:], in1=cosT[:], op=ALU.mult), reads=[pt, T_tab], writes=[ta])
            P.op("dve", lambda e: e.stream_shuffle(out=b[:], in_=pb[:], mask=[(i + 16) % 32 for i in range(32)]), reads=[pt], writes=[tb])
            P.op("dve", lambda e: e.tensor_tensor(out=b[:], in0=b[:], in1=sinT[:], op=ALU.mult), reads=[T_tab], writes=[tb])
            P.op("dve", lambda e: e.tensor_tensor(out=out_ap, in0=a[0:nparts, :], in1=b[0:nparts, :], op=ALU.add), reads=[ta, tb], writes=out_tiles)

        class _Stop(Exception):
            pass

        def win_layer(l):
            depth = len(ra["stack"])
            try:
                win_layer_(l)
            except _Stop:
                while len(ra["stack"]) > depth:
                    ra_pop()

        def stop_at(n):
            if WIN_STOP == n:
                raise _Stop()

        def win_layer_(l):
            j = l // 2
            W = w_qkv[j]
            A1 = lambda g, ch: AMF[:, l, g, 0, ch:ch + 1]
            B1 = lambda g, ch: modv(l, g, 0, ch)
            ra_push()
            kT_S = ralloc("kT_S", [128, 4, 1024], BF16)
            T_kTS = [Tile("kTS%d" % m) for m in range(4)]
            VA_S = ralloc("VA_S", [128, 8, 8, 128], BF16)
            T_VAS = [Tile("VAS%d" % i) for i in range(8)]
            for (va, tv, n) in ((VA_S, T_VAS, 8),):
                P.op("dve", lambda e, va=va: e.memset(va[:].rearrange("p k g d -> p (k g) d")[:, :, 64:128], 1.0), writes=tv)

            stop_at(-2)
            norm_mod([1], {1: 1}, A1, B1)
            stop_at(-1)
            ra_push()
            rtmp = [ralloc("rt%d" % i, [128, 512], F32) for i in range(4)]
            T_rtmp = [Tile("rt%d" % i) for i in range(4)]
            ring = Ring([0, 1, 2, 3, 4, 5, 6, 7])
            ri = 0
            wkb = [wget([(wview(W, 2048 + b2 * 256, 256), 0, 256)], 16, 256, keep=b2) for b2 in range(2)]
            wvb = [wget([(wview(W, 2560 + b2 * 256, 256), 0, 256)], 16, 256, keep=2 + b2) for b2 in range(2)]
            if wkb[0][0] is not None:
                for m in range(4):
                    wv_, wt = wkb[m // 2]
                    pb, pt = ring.next()
                    items = [(pb[:], wv_[:, k, (m % 2) * 128:(m % 2 + 1) * 128], ht_ap(1, k), k == 0, k == 15) for k in range(16)]
                    P.op("pe", mm_group(items), reads=[wt] + HT[1], writes=[pt])
                    rope_evac(pb, pt, kT_S[:, m, 384:896], [T_kTS[m]], rtmp, T_rtmp, ri)
                    ri += 1
                stop_at(0)
                for tt in range(4):
                    pb, pt = ring.next()
                    items = []
                    for b2 in range(2):
                        wv_, wt = wvb[b2]
                        items += [(pb[:, b2 * 256:(b2 + 1) * 256], ht_ap(1, k)[:, tt * 128:(tt + 1) * 128], wv_[:, k, :], k == 0, k == 15) for k in range(16)]
                    P.op("pe", mm_group(items), reads=[wvb[0][1], wvb[1][1]] + HT[1], writes=[pt])
                    if VVAR == 2:
                        P.op("dve", lambda e, pb=pb, tt=tt: e.tensor_copy(out=VA_S[:, 3 + tt, :, 0:64], in_=pb[:].rearrange("p (g d) -> p g d", g=8)), reads=[pt], writes=[T_VAS[3 + tt]])
                    elif VVAR == 3:
                        P.op("act", lambda e, pb=pb, tt=tt: e.copy(out=rtmp[0][:], in_=pb[:]), reads=[pt], writes=[T_rtmp[0]])
                    elif VVAR != 1:
                        P.op("act", lambda e, pb=pb, tt=tt: e.copy(out=VA_S[:, 3 + tt, :, 0:64], in_=pb[:].rearrange("p (g d) -> p g d", g=8)), reads=[pt], writes=[T_VAS[3 + tt]])
            stop_at(1)
            bw, gw = b_win[j], g_win[j]
            if wkb[0][0] is not None:
                hst = ralloc("hst", [128, 2048], BF16)
                T_hst = Tile("hst")
                for side, c0 in ((0, 384), (1, 768)):
                    P.op("dve", lambda e, side=side, c0=c0: e.tensor_copy(
                        out=hst[:, side * 1024:side * 1024 + 512].rearrange("p (m t) -> p m t", m=4), in_=kT_S[:, :, c0:c0 + 128]),
                        reads=T_kTS, writes=[T_hst])
                    P.op("dve", lambda e, side=side: e.tensor_copy(
                        out=hst[:, side * 1024 + 512:side * 1024 + 1024].rearrange("p (g d) -> p g d", g=8), in_=VA_S[:, 3 + 3 * side, :, 0:64]),
                        reads=[T_VAS[3 + 3 * side]], writes=[T_hst])
                P.dma("sp", bw, hst[:], reads=[T_hst], writes=[T_bwin[j]])
                P.allgather(bw, gw, G4, reads=[T_bwin[j]], writes=[T_gwin[j]])
            stop_at(2)
            for b8 in range(8):
                wv_, wt = wget([(wview(W, b8 * 256, 256), 0, 256)], 16, 256)
                if wv_ is None:
                    continue
                for jj in range(2):
                    ch = b8 * 2 + jj
                    pb, pt = ring.next()
                    items = [(pb[:], wv_[:, k, jj * 128:(jj + 1) * 128], ht_ap(1, k), k == 0, k == 15) for k in range(16)]
                    P.op("pe", mm_group(items), reads=[wt] + HT[1], writes=[pt])
                    rope_evac(pb, pt, ht_ap(0, ch), [HT[0][ch]], rtmp, T_rtmp, ri)
                    ri += 1
            ra_pop()

            stop_at(3)
            ra_push()
            kT_P = ralloc("kT_P", [128, 4, 512], BF16)
            T_kTP = [Tile("kTP%d" % m) for m in range(4)]
            VA_P = ralloc("VA_P", [128, 4, 8, 128], BF16)
            T_VAP = [Tile("VAP%d" % i) for i in range(4)]
            qT_P = ralloc("qT_P", [128, 16, 512], BF16)
            T_qTP = [Tile("qTP%d" % c) for c in range(16)]
            for (va, tv, n) in ((VA_P, T_VAP, 4),):
                P.op("dve", lambda e, va=va: e.memset(va[:].rearrange("p k g d -> p (k g) d")[:, :, 64:128], 1.0), writes=tv)
            norm_mod([0], {0: 1}, A1, B1)
            ra_push()
            stg = [ralloc("stg%d" % i, [128, 512], F32) for i in range(4)]
            T_stg = [Tile("stg%d" % i) for i in range(4)]
            si = 0
            ring = Ring([0, 1, 2, 3, 4, 5, 6, 7])
            wkb = [wget([(wview(W, 2048 + b2 * 256, 256), 0, 256)], 16, 256, keep=b2) for b2 in range(2)]
            wvb = [wget([(wview(W, 2560 + b2 * 256, 256), 0, 256)], 16, 256, keep=2 + b2) for b2 in range(2)]
            if wkb[0][0] is not None:
                for m in range(4):
                    wv_, wt = wkb[m // 2]
                    pb, pt = ring.next()
                    items = [(pb[:], wv_[:, k, (m % 2) * 128:(m % 2 + 1) * 128], ht_ap(1, k), k == 0, k == 15) for k in range(16)]
                    P.op("pe", mm_group(items), reads=[wt] + HT[1], writes=[pt])
                    P.op("act", lambda e, pb=pb, m=m: e.copy(out=kT_P[:, m, :], in_=pb[:]), reads=[pt], writes=[T_kTP[m]])
                for tt in range(4):
                    sq_, tok0 = tt // 2, (tt % 2) * 128
                    for which, wb, odst in ((0, wkb, o_wk), (1, wvb, o_wv)):
                        pb, pt = ring.next()
                        items = []
                        for b2 in range(2):
                            wv_, wt = wb[b2]
                            items += [(pb[:, b2 * 256:(b2 + 1) * 256], ht_ap(1, k)[:, tt * 128:(tt + 1) * 128], wv_[:, k, :], k == 0, k == 15) for k in range(16)]
                        P.op("pe", mm_group(items), reads=[wb[0][1], wb[1][1]] + HT[1], writes=[pt])
                        s_, ts_ = stg[si % 4], T_stg[si % 4]
                        si += 1
                        P.op("act", lambda e, pb=pb, s_=s_: e.copy(out=s_[:], in_=pb[:]), reads=[pt], writes=[ts_])
                        P.dma("sp", odst[sq_, j, tok0:tok0 + 128, :], s_[:], reads=[ts_])
                        if which == 1:
                            P.op("dve", lambda e, pb=pb, tt=tt: e.tensor_copy(out=VA_P[:, tt, :, 0:64], in_=pb[:].rearrange("p (g d) -> p g d", g=8)), reads=[pt], writes=[T_VAP[tt]])
            for b8 in range(8):
                wv_, wt = wget([(wview(W, b8 * 256, 256), 0, 256)], 16, 256)
                if wv_ is None:
                    continue
                for jj in range(2):
                    ch = b8 * 2 + jj
                    pb, pt = ring.next()
                    items = [(pb[:], wv_[:, k, jj * 128:(jj + 1) * 128], ht_ap(1, k), k == 0, k == 15) for k in range(16)]
                    P.op("pe", mm_group(items), reads=[wt] + HT[1], writes=[pt])
                    copy_any("dve" if ch % 2 else "act", qT_P[:, ch, :], pb[:], [pt], [T_qTP[ch]])
            ra_pop()

            stop_at(4)
            def va_lhsT(va, kbi, h):
                return va[:, kbi, h // 4, :]

            def normalize(acc, tacc, ncols, h, out_ap, out_tiles, rd, trd):
                P.op("dve", lambda e: e.tensor_scalar(out=rd[0:64, 0:ncols], in0=acc[64:128, 0:ncols],
                                                       scalar1=esink[64:128, j, h:h + 1], scalar2=None, op0=ALU.add),
                     reads=[tacc, T_es], writes=[trd])
                P.op("dve", lambda e: e.reciprocal(out=rd[0:64, 0:ncols], in_=rd[0:64, 0:ncols]), writes=[trd])
                P.op("dve", lambda e: e.tensor_tensor(out=out_ap, in0=acc[0:64, 0:ncols], in1=rd[0:64, 0:ncols], op=ALU.mult),
                     reads=[tacc, trd], writes=out_tiles)

            ra_push()
            PT = [ralloc("PT%d" % i, [128, 512], BF16) for i in range(4)]
            T_PT = [Tile("PT%d" % i) for i in range(4)]
            rdb = [ralloc("rd%d" % i, [128, 512], F32) for i in range(2)]
            T_rd = [Tile("rd%d" % i) for i in range(2)]
            kalt = [ralloc("kalt%d" % i, [128, 512], BF16) for i in range(2)]
            T_kalt = [Tile("kalt%d" % i) for i in range(2)]
            ring_s = Ring([0, 1, 2, 3])
            ring_a = Ring([4, 5, 6, 7])
            it = 0
            for g in range(8):
                gp = (g % 2) * 64
                ap_ = (1 - g % 2) * 64
                ka, tka = kalt[g % 2], T_kalt[g % 2]
                P.op("dve", lambda e, ka=ka, g=g, gp=gp, ap_=ap_: e.tensor_copy(out=ka[ap_:ap_ + 64, :], in_=kT_P[gp:gp + 64, g // 2, :]),
                     reads=[T_kTP[g // 2]], writes=[tka])
                for hi in range(4):
                    h = 4 * g + hi
                    hp = (h % 2) * 64
                    for s in range(2):
                        pbs, pts = ring_s.next()
                        items = []
                        for kb in range(2):
                            c0 = s * 256 + kb * 128
                            if hp == gp:
                                kl = kT_P[hp:hp + 64, g // 2, c0:c0 + 128]
                            else:
                                kl = ka[hp:hp + 64, c0:c0 + 128]
                            items.append((pbs[:, kb * 256:(kb + 1) * 256], kl, qT_P[hp:hp + 64, h // 2, s * 256:(s + 1) * 256], True, True))
                        P.op("pe", mm_group(items), reads=[T_kTP[g // 2], tka, T_qTP[h // 2]], writes=[pts])
                        pt_, tpt = PT[it % 4], T_PT[it % 4]
                        P.op("act", act_fn(pt_[:], pbs[:], AF.Exp, scale=WIN_SCALE), reads=[pts], writes=[tpt])
                        pba, pta = ring_a.next()
                        items = [(pba[:, 0:256], va_lhsT(VA_P, 2 * s + kb, h), pt_[:, kb * 256:(kb + 1) * 256], kb == 0, kb == 1) for kb in range(2)]
                        P.op("pe", mm_group(items), reads=[tpt, T_VAP[2 * s], T_VAP[2 * s + 1]], writes=[pta])
                        normalize(pba, pta, 256, h, hT[hp:hp + 64, 1, h // 2, s * 256:(s + 1) * 256], [HT[1][h // 2]], rdb[it % 2], T_rd[it % 2])
                        it += 1
            ra_pop()
            stop_at(5)
            out_proj(w_wo[j], 0, 1, lambda dc: modv(l, 0, 2, dc))
            ra_pop()
            stop_at(6)

            ra_push()
            cst = [ralloc("cst%d" % i, [128, 512], F32) for i in range(2)]
            T_cst = [Tile("cst%d" % i) for i in range(2)]
            ring = Ring([0, 1, 2, 3])
            for tt in range(2):
                P.dma("sp", cst[tt][:], c_wk[j, tt * 128:(tt + 1) * 128, :], writes=[T_cst[tt]])
                pb, pt = ring.next()
                items = [(pb[:, m * 128:(m + 1) * 128], cst[tt][:, m * 128:(m + 1) * 128], ident_f[:]) for m in range(4)]
                P.op("pe", lambda e, items=items: [e.transpose(o, i, idn) for (o, i, idn) in items][-1], reads=[T_cst[tt], T_const], writes=[pt])
                P.op("dve", lambda e, pb=pb, tt=tt: e.tensor_copy(out=kT_S[:, :, tt * 128:(tt + 1) * 128], in_=pb[:].rearrange("p (m t) -> p m t", m=4)),
                     reads=[pt], writes=T_kTS)
                P.dma("pool", VA_S[:, tt, :, 0:64], c_wv[j, tt * 128:(tt + 1) * 128, :].rearrange("p (g d) -> p g d", g=8), writes=[T_VAS[tt]])
            gbuf = [ralloc("gbuf%d" % i, [128, 2048], BF16) for i in range(2)]
            T_gb = [Tile("gbuf%d" % i) for i in range(2)]
            first = {0: True, 1: True}
            gi = 0
            for r in range(4):
                gb, tgb = gbuf[gi % 2], T_gb[gi % 2]
                gi += 1
                P.dma("sp", gb[:], gw[r * 128:(r + 1) * 128, :], reads=[T_gwin[j]], writes=[tgb])
                for side in range(2):
                    srcside = 1 - side
                    if (side == 0 and r == 3) or (side == 1 and r == 0):
                        continue
                    selc = selT[:, side * 4 + r:side * 4 + r + 1]
                    kdst = kT_S[:, :, 256:384] if side == 0 else kT_S[:, :, 896:1024]
                    ksrc = gb[:, srcside * 1024:srcside * 1024 + 512].rearrange("p (m t) -> p m t", m=4)
                    vdst = VA_S[:, 2 if side == 0 else 7, :, 0:64]
                    vsrc = gb[:, srcside * 1024 + 512:srcside * 1024 + 1024].rearrange("p (g d) -> p g d", g=8)
                    tv = T_VAS[2 if side == 0 else 7]
                    if first[side]:
                        P.op("dve", lambda e, kdst=kdst, ksrc=ksrc, selc=selc: e.tensor_scalar(out=kdst, in0=ksrc, scalar1=selc, scalar2=None, op0=ALU.mult),
                             reads=[tgb, T_tab], writes=T_kTS)
                        P.op("dve", lambda e, vdst=vdst, vsrc=vsrc, selc=selc: e.tensor_scalar(out=vdst, in0=vsrc, scalar1=selc, scalar2=None, op0=ALU.mult),
                             reads=[tgb, T_tab], writes=[tv])
                        first[side] = False
                    else:
                        for m in range(4):
                            P.op("dve", lambda e, kdst=kdst, ksrc=ksrc, selc=selc, m=m: e.scalar_tensor_tensor(
                                out=kdst[:, m, :], in0=ksrc[:, m, :], scalar=selc, in1=kdst[:, m, :], op0=ALU.mult, op1=ALU.add),
                                reads=[tgb, T_tab], writes=T_kTS)
                        P.op("dve", lambda e, vdst=vdst, vsrc=vsrc, selc=selc: e.scalar_tensor_tensor(
                            out=vdst, in0=vsrc, scalar=selc, in1=vdst, op0=ALU.mult, op1=ALU.add), reads=[tgb, T_tab], writes=[tv])
            stop_at(7)
            PT = [ralloc("PTs%d" % i, [128, 512], BF16) for i in range(6)]
            T_PT = [Tile("PTs%d" % i) for i in range(6)]
            rdb = [ralloc("rds%d" % i, [128, 512], F32) for i in range(2)]
            T_rd = [Tile("rds%d" % i) for i in range(2)]
            kalt = [ralloc("kalts%d" % i, [128, 1024], BF16) for i in range(2)]
            T_kalt = [Tile("kalts%d" % i) for i in range(2)]
            ring_s = Ring([0, 1, 2, 3, 4, 5])
            ring_a = Ring([6, 7])
            pi = 0
            hi_ = 0
            kb_list = [(0, 0, 4), (1, 0, 4)] + [(kbi, max(0, kbi - 4), min(3, kbi - 2) + 1) for kbi in range(2, 8)]
            for g in range(8):
                gp = (g % 2) * 64
                ap_ = (1 - g % 2) * 64
                ka, tka = kalt[g % 2], T_kalt[g % 2]
                P.op("dve", lambda e, ka=ka, g=g, gp=gp, ap_=ap_: e.tensor_copy(out=ka[ap_:ap_ + 64, :], in_=kT_S[gp:gp + 64, g // 2, :]),
                     reads=[T_kTS[g // 2]], writes=[tka])
                for hh_ in range(4):
                    h = 4 * g + hh_
                    hp = (h % 2) * 64
                    pba, pta = ring_a.next()
                    pend = None
                    for idx, (kbi, q0, q1) in enumerate(kb_list):
                        nq = q1 - q0
                        pbs, pts = ring_s.next()
                        c0 = kbi * 128
                        if hp == gp:
                            kl = kT_S[hp:hp + 64, g // 2, c0:c0 + 128]
                        else:
                            kl = ka[hp:hp + 64, c0:c0 + 128]
                        items = [(pbs[:, 0:nq * 128], kl, hT[hp:hp + 64, 0, h // 2, q0 * 128:q1 * 128], True, kbi < 2)]
                        if kbi >= 2:
                            kb = kbi - 3
                            for qb in range(q0, q1):
                                if qb == kb:
                                    continue
                                if kbi == 2:
                                    mi = 2
                                elif kbi == 7:
                                    mi = 3
                                else:
                                    mi = 0 if qb == kb + 1 else 1
                                items.append((pbs[:, (qb - q0) * 128:(qb - q0 + 1) * 128], ident_b[:], mk[:, mi, :], False, False))
                            o_, l_, r_, st_, _ = items[-1]
                            items[-1] = (o_, l_, r_, st_, True)
                        P.op("pe", mm_group(items), reads=[T_kTS[g // 2], tka, HT[0][h // 2], T_const, T_tab], writes=[pts])
                        pt_, tpt = PT[pi % 6], T_PT[pi % 6]
                        pi += 1
                        P.op("act", act_fn(pt_[:, 0:nq * 128], pbs[:, 0:nq * 128], AF.Exp, scale=WIN_SCALE), reads=[pts], writes=[tpt])
                        cur = (pba[:, q0 * 128:q1 * 128], va_lhsT(VA_S, kbi, h), pt_[:, 0:nq * 128], idx == 0, idx == len(kb_list) - 1, tpt, T_VAS[kbi])
                        if pend is not None:
                            P.op("pe", mm_group([pend[0:5]]), reads=[pend[5], pend[6]], writes=[pta])
                        pend = cur
                    P.op("pe", mm_group([pend[0:5]]), reads=[pend[5], pend[6]], writes=[pta])
                    normalize(pba, pta, 512, h, hT[hp:hp + 64, 1, h // 2, :], [HT[1][h // 2]], rdb[hi_ % 2], T_rd[hi_ % 2])
                    hi_ += 1
            ra_pop()
            out_proj(w_wo[j], 1, 1, lambda dc: modv(l, 1, 2, dc))
            ra_pop()
            P.fence()

        def mla_layer(l):
            j = l // 2
            W = m_in[j]
            A1 = lambda g, ch: AMF[:, l, g, 0, ch:ch + 1]
            B1 = lambda g, ch: modv(l, g, 0, ch)
            ra_push()
            cqn = {1: ralloc("cqn1", [128, 4, 512], BF16)}
            T_cqn = {1: Tile("cqn1")}
            ckvT = ralloc("ckvT", [128, 4, 2304], BF16)
            T_ckv = Tile("ckvT")
            kpT = ralloc("kpT", [128, 2304], BF16)
            T_kp = Tile("kpT")

            def w_in_proj(g, hh):
                ra_push()
                rs2 = ralloc("rs2", [128, 512], F32)
                T_rs2 = Tile("rs2")
                rs3 = ralloc("rs3", [128, 512], F32)
                T_rs3 = Tile("rs3")
                sq = [ralloc("msq%d" % i, [128, 512], BF16) for i in range(2)]
                T_sq = [Tile("msq%d" % i) for i in range(2)]
                rtmp = [ralloc("mrt%d" % i, [128, 512], F32) for i in range(2)]
                T_rtmp = [Tile("mrt%d" % i) for i in range(2)]
                ckf = ralloc("ckf", [128, 4, 512], F32) if g == 0 else None
                T_ckf = Tile("ckf")
                stb = ralloc("stb", [128, 2560], BF16) if g == 1 else None
                T_stb = Tile("stb")
                for part in range(2):
                    wblk = [wget([(wview(W, (part * 2 + b) * 256, 256), 0, 256)], 16, 256, keep=b) for b in range(2)]
                    if wblk[0][0] is None:
                        continue
                    banks = []
                    for m in range(4):
                        wv_, wt = wblk[m // 2]
                        pb, pt = PB[m], PBT[m]
                        items = [(pb[:], wv_[:, k, (m % 2) * 128:(m % 2 + 1) * 128], ht_ap(hh, k), k == 0, k == 15) for k in range(16)]
                        P.op("pe", mm_group(items), reads=[wt] + HT[hh], writes=[pt])
                        banks.append((pb, pt))
                    pbs, pts = PB[4 + part], PBT[4 + part]
                    for m in range(4):
                        P.op("act", act_fn(sq[m % 2][:], banks[m][0][:], AF.Square), reads=[banks[m][1]], writes=[T_sq[m % 2]])
                        P.op("pe", mm_group([(pbs[:], ones_b[:], sq[m % 2][:], m == 0, m == 3)]), reads=[T_sq[m % 2], T_const], writes=[pts])
                    P.op("act", act_fn(rs2[:], pbs[:], AF.Sqrt, scale=1.0 / 512, bias=EPS), reads=[pts], writes=[T_rs2])
                    P.op("dve", lambda e: e.reciprocal(out=rs3[:], in_=rs2[:]), reads=[T_rs2], writes=[T_rs3])
                    for m in range(4):
                        pb, pt = banks[m]
                        nv = nT[:, m, (0 if part == 0 else 2) + j:(0 if part == 0 else 2) + j + 1]
                        if part == 0:
                            P.op("dve", lambda e, pb=pb, m=m, nv=nv: e.scalar_tensor_tensor(
                                out=cqn[g][:, m, :], in0=pb[:], scalar=nv, in1=rs3[:], op0=ALU.mult, op1=ALU.mult),
                                reads=[pt, T_rs3, T_vT], writes=[T_cqn[g]])
                        elif g == 0:
                            P.op("dve", lambda e, pb=pb, m=m, nv=nv: e.scalar_tensor_tensor(
                                out=ckf[:, m, :], in0=pb[:], scalar=nv, in1=rs3[:], op0=ALU.mult, op1=ALU.mult),
                                reads=[pt, T_rs3, T_vT], writes=[T_ckf])
                            P.op("act", lambda e, m=m: e.copy(out=ckvT[:, m, 0:512], in_=ckf[:, m, :]), reads=[T_ckf], writes=[T_ckv])
                        else:
                            P.op("dve", lambda e, pb=pb, m=m, nv=nv: e.scalar_tensor_tensor(
                                out=stb[:, m * 512:(m + 1) * 512], in0=pb[:], scalar=nv, in1=rs3[:], op0=ALU.mult, op1=ALU.mult),
                                reads=[pt, T_rs3, T_vT], writes=[T_stb])
                wpe = wget([(wview(W, 1024, 64), 0, 64), (wview(W, 1024, 64), 64, 64)], 16, 128)
                if wpe[0] is None:
                    ra_pop()
                    return
                wv_, wt = wpe
                pb, pt = PB[6], PBT[6]
                items = [(pb[:], wv_[:, k, :], ht_ap(hh, k), k == 0, k == 15) for k in range(16)]
                P.op("pe", mm_group(items), reads=[wt] + HT[hh], writes=[pt])
                if g == 1:
                    rope_evac(pb, pt, stb[:, 2048:2560], [T_stb], rtmp, T_rtmp, 0)
                    P.dma("sp", b_mla[j], stb[:], reads=[T_stb], writes=[T_bmla[j]])
                    P.allgather(b_mla[j], g_mla[j], G4, reads=[T_bmla[j]], writes=[T_gmla[j]])
                else:
                    kpf = rtmp[0]
                    P.op("act", lambda e: e.copy(out=kpf[:], in_=pb[:]), reads=[pt], writes=[T_rtmp[0]])
                    P.op("dve", lambda e: e.tensor_copy(out=kpT[:, 0:512], in_=pb[:]), reads=[pt], writes=[T_kp])
                    stg = [ralloc("mstg%d" % i, [128, 512], F32) for i in range(2)]
                    T_stg = [Tile("mstg%d" % i) for i in range(2)]
                    stk = ralloc("mstk", [128, 4, 64], F32)
                    T_stk = Tile("mstk")
                    pbk, ptk = PB[7], PBT[7]
                    items = [(pbk[:, tt * 64:(tt + 1) * 64], kpf[0:64, tt * 128:(tt + 1) * 128], ident_f[0:64, 0:64]) for tt in range(4)]
                    P.op("pe", lambda e, items=items: [e.transpose(o, i, idn) for (o, i, idn) in items][-1], reads=[T_rtmp[0], T_const], writes=[ptk])
                    P.op("dve", lambda e: e.tensor_copy(out=stk[:], in_=pbk[:, 0:256].rearrange("p (t f) -> p t f", t=4)), reads=[ptk], writes=[T_stk])
                    for tt in range(4):
                        P.dma("sp", o_kpe[tt // 2, j, (tt % 2) * 128:(tt % 2 + 1) * 128, :], stk[:, tt, :], reads=[T_stk])
                        pb2, pt2 = PB[tt % 4], PBT[tt % 4]
                        items = [(pb2[:, m * 128:(m + 1) * 128], ckf[:, m, tt * 128:(tt + 1) * 128], ident_f[:]) for m in range(4)]
                        P.op("pe", lambda e, items=items: [e.transpose(o, i, idn) for (o, i, idn) in items][-1], reads=[T_ckf, T_const], writes=[pt2])
                        P.op("act", lambda e, pb2=pb2, tt=tt: e.copy(out=stg[tt % 2][:], in_=pb2[:]), reads=[pt2], writes=[T_stg[tt % 2]])
                        P.dma("sp", o_ckv[tt // 2, j, (tt % 2) * 128:(tt % 2 + 1) * 128, :], stg[tt % 2][:], reads=[T_stg[tt % 2]])
                ra_pop()

            def attention(g, hh, nkeys, qblocks):
                ra_push()
                nkt = nkeys // 128
                qn = [ralloc("qn%d" % i, [128, 512], BF16) for i in range(2)]
                qp = [ralloc("qp%d" % i, [128, 512], BF16) for i in range(2)]
                kn = [ralloc("kn%d" % i, [128, nkeys], BF16) for i in range(2)]
                Vh = [ralloc("Vh%d" % i, [128, nkt, 128], BF16) for i in range(2)]
                T_qn = [Tile("qn%d" % i) for i in range(2)]
                T_qp = [Tile("qp%d" % i) for i in range(2)]
                T_kn = [Tile("kn%d" % i) for i in range(2)]
                T_Vh = [Tile("Vh%d" % i) for i in range(2)]
                PT = [ralloc("mPT%d" % i, [128, 512], BF16) for i in range(4)]
                T_PT = [Tile("mPT%d" % i) for i in range(4)]
                rd = [ralloc("mrd%d" % i, [128, 512], F32) for i in range(2)]
                T_rd = [Tile("mrd%d" % i) for i in range(2)]
                rtmp = [ralloc("art%d" % i, [128, 512], F32) for i in range(2)]
                T_rtmp = [Tile("art%d" % i) for i in range(2)]
                ring_x = Ring([0, 1])
                ring_s = Ring([2, 3, 4])
                pi = 0
                for h in range(16):
                    b = h % 2
                    wq_, wqt = wget([(m_qb[j].rearrange("(k p) n -> p k n", p=128)[:, :, h * 192:(h + 1) * 192], 0, 192)], 4, 192)
                    wk_, wkt = wget([(m_kvb[j].rearrange("(k p) n -> p k n", p=128)[:, :, h * 256:(h + 1) * 256], 0, 256)], 4, 256, keep=1)
                    if wq_ is None:
                        continue
                    pb, pt = ring_x.next()
                    P.op("pe", mm_group([(pb[:], wq_[:, k, 0:128], cqn[g][:, k, :], k == 0, k == 3) for k in range(4)]), reads=[wqt, T_cqn[g]], writes=[pt])
                    P.op("act", lambda e, pb=pb, b=b: e.copy(out=qn[b][:], in_=pb[:]), reads=[pt], writes=[T_qn[b]])
                    pb, pt = ring_x.next()
                    P.op("pe", mm_group([(pb[0:64, :], wq_[:, k, 128:192], cqn[g][:, k, :], k == 0, k == 3) for k in range(4)]), reads=[wqt, T_cqn[g]], writes=[pt])
                    if g == 1:
                        a_, ta_ = rtmp[0], T_rtmp[0]
                        b_, tb_ = rtmp[1], T_rtmp[1]
                        P.op("dve", lambda e, pb=pb, a_=a_: e.tensor_tensor(out=a_[0:64, :], in0=pb[0:64, :], in1=cosT[0:64, :], op=ALU.mult), reads=[pt, T_tab], writes=[ta_])
                        P.op("dve", lambda e, pb=pb, b_=b_: e.stream_shuffle(out=b_[0:64, :], in_=pb[0:64, :], mask=[(i + 16) % 32 for i in range(32)]), reads=[pt], writes=[tb_])
                        P.op("dve", lambda e, b_=b_: e.tensor_tensor(out=b_[0:64, :], in0=b_[0:64, :], in1=sinT[0:64, :], op=ALU.mult), reads=[T_tab], writes=[tb_])
                        P.op("dve", lambda e, a_=a_, b_=b_, b=b: e.tensor_tensor(out=qp[b][0:64, :], in0=a_[0:64, :], in1=b_[0:64, :], op=ALU.add), reads=[ta_, tb_], writes=[T_qp[b]])
                    else:
                        P.op("dve", lambda e, pb=pb, b=b: e.tensor_copy(out=qp[b][0:64, :], in_=pb[0:64, :]), reads=[pt], writes=[T_qp[b]])
                    for c0 in range(0, nkeys, 512):
                        n = min(512, nkeys - c0)
                        pb, pt = ring_x.next()
                        P.op("pe", mm_group([(pb[:, 0:n], wk_[:, k, 0:128], ckvT[:, k, c0:c0 + n], k == 0, k == 3) for k in range(4)]), reads=[wkt, T_ckv], writes=[pt])
                        copy_any("act" if (c0 // 512) % 2 else "dve", kn[b][:, c0:c0 + n], pb[:, 0:n], [pt], [T_kn[b]])
                    for k0 in range(0, nkt, 4):
                        nn = min(4, nkt - k0)
                        pb, pt = ring_x.next()
                        items = []
                        for kk in range(nn):
                            kt = k0 + kk
                            items += [(pb[:, kk * 128:(kk + 1) * 128], ckvT[:, k, kt * 128:(kt + 1) * 128], wk_[:, k, 128:256], k == 0, k == 3) for k in range(4)]
                        P.op("pe", mm_group(items), reads=[wkt, T_ckv], writes=[pt])
                        copy_any("dve", Vh[b][:, k0:k0 + nn, :], pb[:, 0:nn * 128].rearrange("p (t f) -> p t f", t=nn), [pt], [T_Vh[b]])
                    for (qc0, qn_, kc0, nk) in qblocks:
                        pba, pta = PB[5], PBT[5]
                        pbd, ptd = PB[6], PBT[6]
                        nkb = nk // 128
                        pend = None
                        for kb in range(nkb):
                            kc = kc0 + kb * 128
                            pbs, pts = ring_s.next()
                            items = [(pbs[:, 0:qn_], kn[b][:, kc:kc + 128], qn[b][:, qc0:qc0 + qn_], True, False),
                                     (pbs[:, 0:qn_], kpT[0:64, kc:kc + 128], qp[b][0:64, qc0:qc0 + qn_], False, True)]
                            P.op("pe", mm_group(items), reads=[T_kn[b], T_qn[b], T_kp, T_qp[b]], writes=[pts])
                            pt_, tpt = PT[pi % 4], T_PT[pi % 4]
                            pi += 1
                            P.op("act", act_fn(pt_[:, 0:qn_], pbs[:, 0:qn_], AF.Exp, scale=MLA_SCALE), reads=[pts], writes=[tpt])
                            cur = (kb, kc // 128, pt_, tpt)
                            if pend is not None:
                                pk, pkt, ppt, ptpt = pend
                                P.op("pe", mm_group([(pba[:, 0:qn_], Vh[b][:, pkt, :], ppt[:, 0:qn_], pk == 0, False)]), reads=[ptpt, T_Vh[b]], writes=[pta])
                                P.op("pe", mm_group([(pbd[:, 0:qn_], ones_b[:], ppt[:, 0:qn_], pk == 0, False)]), reads=[ptpt, T_const], writes=[ptd])
                            pend = cur
                        pk, pkt, ppt, ptpt = pend
                        P.op("pe", mm_group([(pba[:, 0:qn_], Vh[b][:, pkt, :], ppt[:, 0:qn_], pk == 0, True)]), reads=[ptpt, T_Vh[b]], writes=[pta])
                        P.op("pe", mm_group([(pbd[:, 0:qn_], ones_b[:], ppt[:, 0:qn_], pk == 0, True)]), reads=[ptpt, T_const], writes=[ptd])
                        r_, tr_ = rd[h % 2], T_rd[h % 2]
                        P.op("dve", lambda e, r_=r_, pbd=pbd, qn_=qn_: e.reciprocal(out=r_[:, 0:qn_], in_=pbd[:, 0:qn_]), reads=[ptd], writes=[tr_])
                        P.op("dve", lambda e, r_=r_, pba=pba, qn_=qn_, qc0=qc0, h=h: e.tensor_tensor(
                            out=ht_ap(hh, h)[:, qc0:qc0 + qn_], in0=pba[:, 0:qn_], in1=r_[:, 0:qn_], op=ALU.mult),
                            reads=[pta, tr_], writes=[HT[hh][h]])
                ra_pop()

            norm_mod([1], {1: 1}, A1, B1)
            w_in_proj(1, 1)
            ra_push()
            cqn[0] = ralloc("cqn0", [128, 4, 512], BF16)
            T_cqn[0] = Tile("cqn0")
            norm_mod([0], {0: 0}, A1, B1)
            w_in_proj(0, 0)
            attention(0, 0, 512, [(0, 256, 0, 256), (256, 256, 256, 256)])
            out_proj(m_wo[j], 0, 0, lambda dc: modv(l, 0, 2, dc))
            ra_pop()
            ra_push()
            ra_push()
            T_gl = Tile("gl")
            for r in range(4):
                P.dma("sp", ckvT[:, :, r * 512:(r + 1) * 512], g_mla[j][r * 128:(r + 1) * 128, 0:2048].rearrange("p (m t) -> p m t", m=4),
                      reads=[T_gmla[j]], writes=[T_ckv])
                P.dma("sp", kpT[:, r * 512:(r + 1) * 512], g_mla[j][r * 128:(r + 1) * 128, 2048:2560], reads=[T_gmla[j]], writes=[T_kp])
            cst = [ralloc("mcst%d" % i, [128, 512], F32) for i in range(2)]
            csk = [ralloc("mcsk%d" % i, [128, 128], F32) for i in range(2)]
            T_cst = [Tile("mcst%d" % i) for i in range(2)]
            T_csk = [Tile("mcsk%d" % i) for i in range(2)]
            for tt in range(2):
                P.dma("sp", cst[tt][:], c_ckv[j, tt * 128:(tt + 1) * 128, :], writes=[T_cst[tt]])
                P.dma("sp", csk[tt][:, 0:64], c_kpe[j, tt * 128:(tt + 1) * 128, :], writes=[T_csk[tt]])
                P.dma("sp", csk[tt][:, 64:128], c_kpe[j, tt * 128:(tt + 1) * 128, :], writes=[T_csk[tt]])
                pb, pt = PB[tt], PBT[tt]
                items = [(pb[:, m * 128:(m + 1) * 128], cst[tt][:, m * 128:(m + 1) * 128], ident_f[:]) for m in range(4)]
                P.op("pe", lambda e, items=items: [e.transpose(o, i, idn) for (o, i, idn) in items][-1], reads=[T_cst[tt], T_const], writes=[pt])
                P.op("dve", lambda e, pb=pb, tt=tt: e.tensor_copy(out=ckvT[:, :, 2048 + tt * 128:2048 + (tt + 1) * 128], in_=pb[:].rearrange("p (m t) -> p m t", m=4)),
                     reads=[pt], writes=[T_ckv])
                pb, pt = PB[2 + tt], PBT[2 + tt]
                P.op("pe", lambda e, pb=pb, tt=tt: e.transpose(pb[:, 0:128], csk[tt][:, :], ident_f[:]), reads=[T_csk[tt], T_const], writes=[pt])
                P.op("dve", lambda e, pb=pb, tt=tt: e.tensor_copy(out=kpT[:, 2048 + tt * 128:2048 + (tt + 1) * 128], in_=pb[:, 0:128]), reads=[pt], writes=[T_kp])
            ra_pop()
            attention(1, 1, 2304, [(0, 512, 0, 2304)])
            ra_pop()
            out_proj(m_wo[j], 1, 1, lambda dc: modv(l, 1, 2, dc))
            ra_pop()
            P.fence()

        def ffn_layer(l):
            A2 = lambda g, ch: AMF[:, l, g, 1, ch:ch + 1]
            B2 = lambda g, ch: modv(l, g, 3, ch)
            norm_mod([0, 1], {0: 0, 1: 1}, A2, B2)
            ra_push()
            aT = ralloc("aT", [128, 6, 1024], BF16)
            T_aT = [[Tile("aT%d_%d" % (k, g)) for g in range(2)] for k in range(6)]
            sg = [ralloc("sg%d" % i, [128, 512], F32) for i in range(3)]
            T_sg = [Tile("sg%d" % i) for i in range(3)]
            ring = Ring(range(8))
            sgi = 0
            sbs = [6, 6, 6, 6, 5, 5, 5, 5]
            fc0 = 0
            WG, WU, WD = f_g[l], f_u[l], f_d[l]
            for nsb in sbs:
                done = 0
                while done < nsb:
                    nb = min(2, nsb - done)
                    c0 = (fc0 + done) * 128
                    wg_, wgt = wget([(wview(WG, c0, nb * 128), 0, nb * 128)], 16, nb * 128)
                    wu_, wut = wget([(wview(WU, c0, nb * 128), 0, nb * 128)], 16, nb * 128, keep=1)
                    if wg_ is not None:
                        for cc in range(nb):
                            kl = done + cc
                            for g in range(2):
                                pbg, ptg = ring.next()
                                P.op("pe", mm_group([(pbg[:], wg_[:, k, cc * 128:(cc + 1) * 128], ht_ap(g, k), k == 0, k == 15) for k in range(16)]),
                                     reads=[wgt] + HT[g], writes=[ptg])
                                pbu, ptu = ring.next()
                                P.op("pe", mm_group([(pbu[:], wu_[:, k, cc * 128:(cc + 1) * 128], ht_ap(g, k), k == 0, k == 15) for k in range(16)]),
                                     reads=[wut] + HT[g], writes=[ptu])
                                s_, ts_ = sg[sgi % 3], T_sg[sgi % 3]
                                sgi += 1
                                P.op("act", act_fn(s_[:], pbg[:], AF.Silu), reads=[ptg], writes=[ts_])
                                P.op("dve", lambda e, s_=s_, pbu=pbu, kl=kl, g=g: e.tensor_tensor(
                                    out=aT[:, kl, g * 512:(g + 1) * 512], in0=pbu[:], in1=s_[:], op=ALU.mult),
                                    reads=[ptu, ts_], writes=[T_aT[kl][g]])
                    done += nb
                for b4 in range(4):
                    src = WD.rearrange("(k p) n -> p k n", p=128)[:, fc0:fc0 + nsb, b4 * 512:(b4 + 1) * 512]
                    wd_, wdt = wget([(src, 0, 512)], nsb, 512)
                    if wd_ is None:
                        continue
                    for cc in range(4):
                        dc = b4 * 4 + cc
                        for g in range(2):
                            pb, pt = ring.next()
                            P.op("pe", mm_group([(pb[:], wd_[:, k, cc * 128:(cc + 1) * 128], aT[:, k, g * 512:(g + 1) * 512], k == 0, k == nsb - 1) for k in range(nsb)]),
                                 reads=[wdt] + [T_aT[k][g] for k in range(nsb)], writes=[pt])
                            P.op("dve", lambda e, pb=pb, dc=dc, g=g: e.scalar_tensor_tensor(
                                out=xt_ap(dc, g), in0=pb[:], scalar=modv(l, g, 5, dc), in1=xt_ap(dc, g), op0=ALU.mult, op1=ALU.add),
                                reads=[pt, T_MOD], writes=[XT[dc][g]])
                fc0 += nsb
            ra_pop()
            P.fence()

        for l in LAYERS:
            if DO_MIX:
                if l % 2 == 0:
                    win_layer(l)
                else:
                    mla_layer(l)
            if DO_FFN:
                ffn_layer(l)

        ra_push()
        sq = [ralloc("fsq%d" % i, [128, 512], BF16) for i in range(4)]
        T_sq = [Tile("fsq%d" % i) for i in range(4)]
        rs = ralloc("frs", [128, 512], F32)
        T_rs = Tile("frs")
        ost = [ralloc("ost%d" % i, [128, D], F32) for i in range(2)]
        T_ost = [Tile("ost%d" % i) for i in range(2)]
        ring = Ring([2, 3, 4, 5, 6, 7])
        oi = 0
        for g, dst in ((0, yp), (1, ys)):
            pbs, pts = PB[g], PBT[g]
            for ch in range(16):
                s, st_ = sq[ch % 4], T_sq[ch % 4]
                P.op("act", act_fn(s[:], xt_ap(ch, g), AF.Square), reads=[XT[ch][g]], writes=[st_])
                P.op("pe", mm_group([(pbs[:], ones_b[:], s[:], ch == 0, ch == 15)]), reads=[st_, T_const], writes=[pts])
            P.op("act", act_fn(rs[:], pbs[:], AF.Sqrt, scale=1.0 / D, bias=EPS), reads=[pts], writes=[T_rs])
            P.op("dve", lambda e, pbs=pbs: e.reciprocal(out=pbs[:], in_=rs[:]), reads=[T_rs], writes=[pts])
            for ch in range(16):
                P.op("dve", lambda e, ch=ch, g=g, pbs=pbs: e.scalar_tensor_tensor(
                    out=xt_ap(ch, g), in0=xt_ap(ch, g), scalar=vT[:, ch, 8:9], in1=pbs[:], op0=ALU.mult, op1=ALU.mult),
                    reads=[pts, T_vT], writes=[XT[ch][g]])
            for tt in range(4):
                o_, to_ = ost[oi % 2], T_ost[oi % 2]
                oi += 1
                for q4 in range(4):
                    pb, pt = ring.next()
                    items = [(pb[:, cc * 128:(cc + 1) * 128], xt_ap(q4 * 4 + cc, g)[:, tt * 128:(tt + 1) * 128], ident_f[:]) for cc in range(4)]
                    P.op("pe", lambda e, items=items: [e.transpose(o, i, idn) for (o, i, idn) in items][-1],
                         reads=[XT[q4 * 4 + cc][g] for cc in range(4)] + [T_const], writes=[pt])
                    copy_any("dve" if q4 % 2 else "act", o_[:, q4 * 512:(q4 + 1) * 512], pb[:], [pt], [to_])
                P.dma("sp", dst[tt * 128:(tt + 1) * 128, :], o_[:], reads=[to_])
        ra_pop()

    snap = (mem["p"], ra["p"], uid[0])
    wq["collect"] = True
    real = P
    dummy = _DummyPlan()
    P = dummy
    build_body()
    P = real
    wq["collect"] = False
    wq["i"] = 0
    mem["p"], ra["p"] = snap[0], snap[1]
    ra["stack"] = []
    build_body()
    P.finish()
    P.emit()
    return nc, P


class _DummyPlan:
    def op(self, *a, **k):
        pass

    def add_pending(self, deps):
        pass

    def dma(self, *a, **k):
        pass

    def allgather(self, *a, **k):
        pass

    def fence(self):
        pass


_CACHE = {}


def _get_program():
    if "nc" not in _CACHE:
        nc, P = build()
        _CACHE["nc"] = nc
        _CACHE["plan"] = P
    return _CACHE["nc"]


def kernel(**inputs):
    in_maps = make_in_maps(**inputs)
    nc = _get_program()
    res = run_bass_kernel_spmd(nc, in_maps, core_ids=list(range(8)))
    return assemble(res.results)


def make_in_maps(x_prompt, x_sample, cache_win_k, cache_win_v, cache_mla_ckv, cache_mla_kpe, c,
                 c_ctx, ada_w, ada_b, norm_mix, norm_ffn, win_w_qkv, win_w_o, win_sink,
                 mla_w_in, mla_q_norm, mla_w_q_b, mla_kv_norm, mla_w_kv_b, mla_w_o,
                 ffn_w_gate, ffn_w_up, ffn_w_down, norm_final):
    f = lambda a: np.ascontiguousarray(np.asarray(a, dtype=np.float32))
    x_prompt, x_sample = f(x_prompt), f(x_sample)
    ada_w, ada_b = f(ada_w), f(ada_b)
    nwin, nmla, nffn = _counts()

    def fw(a, n):
        return f(np.asarray(a)[:n]) if n else np.zeros((1, 1, 1), np.float32)

    shared = {
        "win_w_qkv": fw(win_w_qkv, nwin), "win_w_o": fw(win_w_o, nwin), "win_sink": f(win_sink),
        "mla_w_in": fw(mla_w_in, nmla), "mla_w_q_b": fw(mla_w_q_b, nmla), "mla_w_kv_b": fw(mla_w_kv_b, nmla), "mla_w_o": fw(mla_w_o, nmla),
        "ffn_w_gate": fw(ffn_w_gate, nffn), "ffn_w_up": fw(ffn_w_up, nffn), "ffn_w_down": fw(ffn_w_down, nffn),
        "nvec": f(np.concatenate([np.asarray(mla_q_norm), np.asarray(mla_kv_norm)], axis=0)),
    }
    cwk = f(cache_win_k).reshape(2, 2, 256, 512)
    cwv = f(cache_win_v).reshape(2, 2, 256, 512)
    cckv = f(cache_mla_ckv)
    ckpe = f(cache_mla_kpe)
    c = f(c)
    in_maps = []
    for core in range(8):
        b, q = core // 4, core % 4
        cosT, sinT = rope_tables(core)
        mks, sel = mask_tables(core)
        m = dict(shared)
        m["xp"] = x_prompt[2 * core:2 * core + 2].reshape(512, D)
        m["xs"] = np.ascontiguousarray(x_sample[b, q * 512:(q + 1) * 512])
        m["vecs"] = f(np.concatenate([np.asarray(norm_mix), np.asarray(norm_ffn), np.asarray(norm_final)[None], np.asarray(c_ctx)[None], c], axis=0))
        m["ada_w"] = np.ascontiguousarray(ada_w[:, :, core * 1536:(core + 1) * 1536])
        m["ada_b"] = np.ascontiguousarray(ada_b[:, core * 1536:(core + 1) * 1536]).reshape(48, 128)
        m["c_wk"] = cwk[b]
        m["c_wv"] = cwv[b]
        m["c_ckv"] = cckv[b]
        m["c_kpe"] = ckpe[b]
        m["ropec"] = cosT
        m["ropes"] = sinT
        m["masks"] = mks
        m["sel"] = sel
        in_maps.append(m)
    return in_maps


def assemble(R):
    y_prompt = np.stack([R[cr]["yp"].reshape(2, 256, D) for cr in range(8)]).reshape(16, 256, D)
    y_sample = np.stack([R[cr]["ys"] for cr in range(8)]).reshape(2, 2048, D)
    wk = np.concatenate([R[cr]["o_wk"] for cr in range(8)], axis=0).reshape(16, 2, 256, 8, 64)
    wv = np.concatenate([R[cr]["o_wv"] for cr in range(8)], axis=0).reshape(16, 2, 256, 8, 64)
    ckv = np.concatenate([R[cr]["o_ckv"] for cr in range(8)], axis=0)
    kpe = np.concatenate([R[cr]["o_kpe"] for cr in range(8)], axis=0)
    return (y_prompt.astype(np.float32), y_sample.astype(np.float32), wk.astype(np.float32), wv.astype(np.float32),
            ckv.astype(np.float32), kpe.astype(np.float32))
```
